# Optimizing a Trainium2 kernel written in Bass

```python
import math
import jax
import jax.numpy as jnp
from jax import lax
import numpy as np

D_MODEL = 2048
BATCH = 1
SEQ = 16384
DEPTH = 4

D_FF = 5504
N_SUB = 3
N_MOD = 3 * N_SUB
A_HEADS = 8
A_QK_DIM = 64
A_V_DIM = 2 * A_QK_DIM
A_WIDTH = A_HEADS * A_V_DIM
B_GROUPS = 8
B_GROUP_DIM = 128
B_WIDTH = B_GROUPS * B_GROUP_DIM
CHUNK = 128
AB_IN = 3 * A_WIDTH + 2 * B_WIDTH
AB_OUT = A_WIDTH + B_WIDTH
C_HEADS = 16
C_KV_HEADS = 4
C_GROUP = C_HEADS // C_KV_HEADS
C_HEAD_DIM = 128
C_WIDTH = C_HEADS * C_HEAD_DIM
C_KV_WIDTH = C_KV_HEADS * C_HEAD_DIM
IDX_HEADS = 16
IDX_DIM = 64
C_IN = C_WIDTH + 2 * C_KV_WIDTH + IDX_HEADS * IDX_DIM + IDX_DIM + IDX_HEADS
TOPK_MAX = 256
REL_BUCKETS = 32
REL_MAX_DIST = 128
REL_HEADS = 16
Q_BLOCK = 128
EPS = 1e-6
N_EVEN = (DEPTH + 1) // 2
N_ODD = DEPTH // 2

kernel_name = 'hybrid_diffattn_sgmlp_dsa_macaron_adaln'


def rms_norm(x, g):
    xf = x.astype(jnp.float32)
    y = xf * lax.rsqrt(jnp.mean(xf * xf, axis=-1, keepdims=True) + EPS)
    return (y * g.astype(jnp.float32)).astype(x.dtype)


def layer_norm(x, g, b):
    xf = x.astype(jnp.float32)
    mu = jnp.mean(xf, axis=-1, keepdims=True)
    var = jnp.mean(jnp.square(xf - mu), axis=-1, keepdims=True)
    y = (xf - mu) * lax.rsqrt(var + EPS)
    return (y * g.astype(jnp.float32) + b.astype(jnp.float32)).astype(x.dtype)


def rel_bucket(dist):
    n = jnp.maximum(dist, 0)
    max_exact = REL_BUCKETS // 2
    nf = jnp.maximum(n, 1).astype(jnp.float32)
    large = max_exact + (jnp.log(nf / max_exact) / math.log(REL_MAX_DIST / max_exact)
                         * (REL_BUCKETS - max_exact)).astype(jnp.int32)
    large = jnp.minimum(large, REL_BUCKETS - 1)
    return jnp.where(n < max_exact, n, large)


def swiglu(h, w1, w2):
    g, u = jnp.split(h @ w1, 2, axis=-1)
    return (jax.nn.silu(g) * u) @ w2


def diff_attention(q, k, v, lam, lam_init, subln_g, rel_table):
    B, S = q.shape[0], q.shape[1]
    nblk = S // Q_BLOCK
    key_pos = jnp.arange(S)
    table = rel_table.reshape(REL_BUCKETS, A_HEADS, 2)
    qb = q.reshape(B, nblk, Q_BLOCK, A_HEADS, 2, A_QK_DIM).swapaxes(0, 1)
    scale = A_QK_DIM ** -0.5

    def block(args):
        qblk, i = args
        qpos = i * Q_BLOCK + jnp.arange(Q_BLOCK)
        dist = qpos[:, None] - key_pos[None, :]
        bias = table[rel_bucket(dist)].transpose(2, 3, 0, 1)
        logits = (jnp.einsum('bqhmd,bshmd->bhmqs', qblk, k).astype(jnp.float32) * scale
                  + bias.astype(jnp.float32))
        logits = jnp.where(dist >= 0, logits, -jnp.inf)
        p = jax.nn.softmax(logits, axis=-1)
        attn = (p[:, :, 0] - lam * p[:, :, 1]).astype(v.dtype)
        return jnp.einsum('bhqs,bshe->bqhe', attn, v)

    out = lax.map(block, (qb, jnp.arange(nblk)))
    out = out.swapaxes(0, 1).reshape(B, S, A_HEADS, A_V_DIM)
    out = rms_norm(out, subln_g) * (1.0 - lam_init)
    return out.reshape(B, S, A_WIDTH)


def chunked_spatial_gating(u, z, ln_g, ln_b, w_s, b_s):
    B, S = u.shape[0], u.shape[1]
    zn = layer_norm(z, ln_g, ln_b)
    zc = zn.reshape(B, S // CHUNK, CHUNK, B_GROUPS, B_GROUP_DIM)
    mask = jnp.tril(jnp.ones((CHUNK, CHUNK), dtype=bool))
    w = jnp.where(mask, w_s, 0)
    sz = jnp.einsum('gts,bnsgc->bntgc', w, zc) + b_s.T[None, None, :, :, None]
    return u * sz.reshape(B, S, B_WIDTH)


def ab_mixer(h, w_in, w_out, lam_p, subln_g, ln_g, ln_b, w_s, b_s, rel_table, layer_idx):
    B, S = h.shape[0], h.shape[1]
    proj = h @ w_in
    qa, ka, va, zb = jnp.split(proj, [A_WIDTH, 2 * A_WIDTH, 3 * A_WIDTH], axis=-1)
    q = qa.reshape(B, S, A_HEADS, 2, A_QK_DIM)
    k = ka.reshape(B, S, A_HEADS, 2, A_QK_DIM)
    v = va.reshape(B, S, A_HEADS, A_V_DIM)
    lam_init = 0.8 - 0.6 * math.exp(-0.3 * layer_idx)
    lp = lam_p.astype(jnp.float32)
    lam = jnp.exp(jnp.sum(lp[0] * lp[1])) - jnp.exp(jnp.sum(lp[2] * lp[3])) + lam_init
    ya = diff_attention(q, k, v, lam, lam_init, subln_g, rel_table)
    u, z = jnp.split(jax.nn.gelu(zb), 2, axis=-1)
    yb = chunked_spatial_gating(u, z, ln_g, ln_b, w_s, b_s)
    return jnp.concatenate([ya, yb], axis=-1) @ w_out


def dsa_mixer(h, w_in, w_out, rel_table):
    B, S = h.shape[0], h.shape[1]
    proj = h @ w_in
    s1 = C_WIDTH
    s2 = s1 + C_KV_WIDTH
    s3 = s2 + C_KV_WIDTH
    s4 = s3 + IDX_HEADS * IDX_DIM
    s5 = s4 + IDX_DIM
    qc, kc, vc, qi, ki, wi = jnp.split(proj, [s1, s2, s3, s4, s5], axis=-1)
    q = qc.reshape(B, S, C_KV_HEADS, C_GROUP, C_HEAD_DIM)
    k = kc.reshape(B, S, C_KV_HEADS, C_HEAD_DIM)
    v = vc.reshape(B, S, C_KV_HEADS, C_HEAD_DIM)
    q_idx = qi.reshape(B, S, IDX_HEADS, IDX_DIM)
    k_idx = ki
    topk = min(TOPK_MAX, S // 4)
    nblk = S // Q_BLOCK
    key_pos = jnp.arange(S)
    gather = jax.vmap(lambda arr, idx: arr[idx])

    def to_blocks(a):
        return a.reshape(B, nblk, Q_BLOCK, *a.shape[2:]).swapaxes(0, 1)

    def block(args):
        qblk, qiblk, wblk, i = args
        qpos = i * Q_BLOCK + jnp.arange(Q_BLOCK)
        sc = jnp.einsum('bqhd,bsd->bqhs', qiblk, k_idx).astype(jnp.float32) * IDX_DIM ** -0.5
        iscore = jnp.einsum('bqhs,bqh->bqs', jax.nn.relu(sc), wblk.astype(jnp.float32))
        iscore = jnp.where(key_pos[None, None, :] <= qpos[None, :, None], iscore, -jnp.inf)
        _, sel = lax.top_k(iscore, topk)
        valid = sel <= qpos[None, :, None]
        k_sel = gather(k, sel)
        v_sel = gather(v, sel)
        bias = rel_table[rel_bucket(qpos[None, :, None] - sel)]
        bias = bias.reshape(B, Q_BLOCK, topk, C_KV_HEADS, C_GROUP).transpose(0, 3, 4, 1, 2)
        logits = (jnp.einsum('bqgrd,bqkgd->bgrqk', qblk, k_sel).astype(jnp.float32)
                  * C_HEAD_DIM ** -0.5 + bias.astype(jnp.float32))
        logits = jnp.where(valid[:, None, None], logits, -jnp.inf)
        p = jax.nn.softmax(logits, axis=-1).astype(v.dtype)
        return jnp.einsum('bgrqk,bqkgd->bqgrd', p, v_sel)

    out = lax.map(block, (to_blocks(q), to_blocks(q_idx), to_blocks(wi), jnp.arange(nblk)))
    out = out.swapaxes(0, 1).reshape(B, S, C_WIDTH)
    return out @ w_out


def setup_inputs(seed: int = 0) -> dict:
    key = jax.random.key(seed)
    ks = jax.random.split(key, 19)

    def nrm(k, shape, scale):
        return jax.random.normal(k, shape, jnp.float32) * scale

    return {
        'x': nrm(ks[0], (BATCH, SEQ, D_MODEL), 1.0),
        'c': nrm(ks[1], (BATCH, D_MODEL), 1.0),
        'norm_g': 1.0 + nrm(ks[2], (DEPTH, N_SUB, D_MODEL), 0.02),
        'mod_w': nrm(ks[3], (DEPTH, D_MODEL, N_MOD * D_MODEL), 0.1 * D_MODEL ** -0.5),
        'mod_b': nrm(ks[4], (DEPTH, N_MOD * D_MODEL), 0.01),
        'ffn_w1': nrm(ks[5], (DEPTH, 2, D_MODEL, 2 * D_FF), D_MODEL ** -0.5),
        'ffn_w2': nrm(ks[6], (DEPTH, 2, D_FF, D_MODEL), D_FF ** -0.5),
        'rel_table': nrm(ks[7], (REL_BUCKETS, REL_HEADS), 0.5),
        'ab_w_in': nrm(ks[8], (N_EVEN, D_MODEL, AB_IN), D_MODEL ** -0.5),
        'ab_w_out': nrm(ks[9], (N_EVEN, AB_OUT, D_MODEL), AB_OUT ** -0.5),
        'diff_lam': nrm(ks[10], (N_EVEN, 4, A_QK_DIM), 0.1),
        'diff_subln_g': 1.0 + nrm(ks[11], (N_EVEN, A_V_DIM), 0.02),
        'sg_ln_g': 1.0 + nrm(ks[12], (N_EVEN, B_WIDTH), 0.02),
        'sg_ln_b': nrm(ks[13], (N_EVEN, B_WIDTH), 0.02),
        'sg_w': nrm(ks[14], (N_EVEN, B_GROUPS, CHUNK, CHUNK), CHUNK ** -0.5),
        'sg_b': 1.0 + nrm(ks[15], (N_EVEN, B_GROUPS, CHUNK), 0.02),
        'dsa_w_in': nrm(ks[16], (N_ODD, D_MODEL, C_IN), D_MODEL ** -0.5),
        'dsa_w_out': nrm(ks[17], (N_ODD, C_WIDTH, D_MODEL), C_WIDTH ** -0.5),
        'final_g': 1.0 + nrm(ks[18], (D_MODEL,), 0.02),
    }


def reference(x, c, norm_g, mod_w, mod_b, ffn_w1, ffn_w2, rel_table, ab_w_in, ab_w_out,
              diff_lam, diff_subln_g, sg_ln_g, sg_ln_b, sg_w, sg_b, dsa_w_in, dsa_w_out,
              final_g):
    B = x.shape[0]
    cs = jax.nn.silu(c)
    for li in range(DEPTH):
        mod = (cs @ mod_w[li] + mod_b[li]).reshape(B, N_MOD, D_MODEL)

        def pre(h, j):
            shift, scale = mod[:, 3 * j], mod[:, 3 * j + 1]
            return rms_norm(h, norm_g[li, j]) * (1.0 + scale[:, None, :]) + shift[:, None, :]

        def gate(j):
            return (1.0 + mod[:, 3 * j + 2])[:, None, :]

        x = x + 0.5 * gate(0) * swiglu(pre(x, 0), ffn_w1[li, 0], ffn_w2[li, 0])
        h = pre(x, 1)
        if li % 2 == 0:
            j = li // 2
            y = ab_mixer(h, ab_w_in[j], ab_w_out[j], diff_lam[j], diff_subln_g[j],
                         sg_ln_g[j], sg_ln_b[j], sg_w[j], sg_b[j], rel_table, li)
        else:
            j = li // 2
            y = dsa_mixer(h, dsa_w_in[j], dsa_w_out[j], rel_table)
        x = x + gate(1) * y
        x = x + 0.5 * gate(2) * swiglu(pre(x, 2), ffn_w1[li, 1], ffn_w2[li, 1])
    return rms_norm(x, final_g)
```

```python
import math
import numpy as np
import ml_dtypes
import concourse.bass as bass
import concourse.mybir as mybir
from concourse.bass_utils import run_bass_kernel_spmd

F32 = mybir.dt.float32
BF16 = mybir.dt.bfloat16
AF = mybir.ActivationFunctionType
ALU = mybir.AluOpType
AX = mybir.AxisListType
NPBF = ml_dtypes.bfloat16

EPS = 1e-6
NEG = -30000.0
SAME_SYNC = True


class Cfg:
    def __init__(s, S=16384, D=2048, DFF=5504, DEPTH=4):
        s.S, s.D, s.DFF, s.DEPTH = S, D, DFF, DEPTH
        s.NCORE = 8
        s.NB = S // 128 // 8
        s.NT = s.NB * 128
        s.T = min(512, s.NT)
        s.NG = s.NT // s.T
        s.TB = s.T // 128
        s.KC = D // 128
        s.FC = DFF // 128
        s.NMC = 9 * D // 8 // 128
        s.NKB = S // 128
        s.TOPK = min(256, S // 4)


class Sem:
    def __init__(s, h):
        s.h = h
        s.val = 0


class Res:
    __slots__ = ("w", "r")

    def __init__(s):
        s.w = None
        s.r = {}


class Q:
    def __init__(s, name, eng, sem):
        s.name, s.eng, s.sem = name, eng, sem
        s.seen = {}

    def wait(s, sem, val):
        if s.seen.get(sem, 0) >= val:
            return
        s.eng.wait_ge(sem.h, val)
        s.seen[sem] = val


class KB:
    def __init__(s, cfg):
        s.cfg = cfg
        s.nc = bass.Bass("TRN2", target_bir_lowering=False)
        nc = s.nc
        s.q = {}
        s.sems = []
        for name, eng in [("pe", nc.tensor), ("act", nc.scalar), ("dve", nc.vector),
                          ("pool", nc.gpsimd), ("sp", nc.sync)]:
            s.q[name] = Q(name, eng, s.sem("q_" + name))
        s.banks = [nc.alloc_psum_tensor("bank%d" % i, [128, 512], F32) for i in range(8)]
        s.bres = [Res() for _ in range(8)]
        s.bset = 0
        s.rr = 0
        s.out_marks = []

    def sem(s, name):
        sm = Sem(s.nc.alloc_semaphore(name))
        s.sems.append(sm)
        return sm

    def sb(s, name, shape, dt):
        return s.nc.alloc_sbuf_tensor(name, list(shape), dt)

    def _deps(s, q, reads, writes, is_pe):
        deps = {}

        def add(sm, v):
            if deps.get(sm, 0) < v:
                deps[sm] = v
        for r in reads:
            if r.w is not None:
                add(*r.w)
        for w in writes:
            if w.w is not None:
                add(*w.w)
            for sm, v in w.r.items():
                add(sm, v)
        for sm, v in deps.items():
            if sm is q.sem and (is_pe or not SAME_SYNC):
                continue
            q.wait(sm, v)

    def _mark(s, m, reads, writes):
        sm, v = m
        for r in reads:
            if r.r.get(sm, 0) < v:
                r.r[sm] = v
        for w in writes:
            w.w = m
            w.r = {}

    def op(s, qn, fn, reads=(), writes=()):
        q = s.q[qn]
        s._deps(q, reads, writes, qn == "pe")
        inst = fn(q.eng)
        q.sem.val += 1
        inst.then_inc(q.sem.h, 1)
        s._mark((q.sem, q.sem.val), reads, writes)

    def dma(s, qn, out, in_, sem, reads=(), writes=(), **kw):
        q = s.q[qn]
        s._deps(q, reads, writes, False)
        inst = q.eng.dma_start(out=out, in_=in_, **kw)
        sem.val += 16
        inst.then_inc(sem.h, 16)
        s._mark((sem, sem.val), reads, writes)

    def next_banks(s):
        s.bset ^= 1
        b0 = 4 * s.bset
        return [b0, b0 + 1, b0 + 2, b0 + 3]

    def ev(s):
        s.rr ^= 1
        return "act" if s.rr else "dve"

    def barrier(s):
        for q in s.q.values():
            for sm in s.sems:
                if sm.val > 0 and sm is not q.sem:
                    q.wait(sm, sm.val)

    def finish(s):
        q = s.q["sp"]
        for sm in s.sems:
            if sm.val > 0 and sm is not q.sem:
                q.wait(sm, sm.val)


class WRing:
    def __init__(s, kb, n=8):
        s.kb = kb
        s.n = n
        s.t = [kb.sb("wr%d" % i, [128, 512], BF16) for i in range(n)]
        s.res = [Res() for _ in range(n)]
        s.sem = [kb.sem("wr%d" % i) for i in range(n)]
        s.i = 0

    def load(s, parts):
        i = s.i
        s.i = (s.i + 1) % s.n
        for c0, n, ap in parts:
            s.kb.dma("pool", s.t[i][:, c0:c0 + n], ap, s.sem[i], writes=[s.res[i]])
        return s.t[i], s.res[i]


def linear(kb, wr, inT, in_res, KCn, T, pieces_fn, mode, nout, evac):
    bs = kb.next_banks()
    TB = T // 128
    for k in range(KCn):
        wt, wres = wr.load(pieces_fn(k))
        st, sp = (k == 0), (k == KCn - 1)
        if mode == "fm":
            for i in range(nout):
                b = bs[i]
                kb.op("pe", lambda e, b=b, i=i: e.matmul(kb.banks[b][:, 0:T], wt[:, i * 128:(i + 1) * 128],
                                                          inT[:, k, 0:T], start=st, stop=sp),
                      reads=[wres, in_res[k]], writes=[kb.bres[b]])
        else:
            for tb in range(TB):
                b = bs[tb]
                kb.op("pe", lambda e, b=b, tb=tb: e.matmul(kb.banks[b][:, 0:nout], inT[:, k, tb * 128:(tb + 1) * 128],
                                                            wt[:, 0:nout], start=st, stop=sp),
                      reads=[wres, in_res[k]], writes=[kb.bres[b]])
    evac(bs)


def gelu_tanh(kb, src, src_res, dst, dst_res, n, tmp, tmp_res):
    t0, t1 = tmp
    r0, r1 = tmp_res
    kb.op("act", lambda e: e.activation(out=t0[:, 0:n], in_=src, func=AF.Square), reads=[src_res], writes=[r0])
    kb.op("dve", lambda e: e.tensor_scalar(out=t0[:, 0:n], in0=t0[:, 0:n], scalar1=0.044715, scalar2=1.0,
                                           op0=ALU.mult, op1=ALU.add), reads=[r0], writes=[r0])
    kb.op("dve", lambda e: e.tensor_tensor(out=t0[:, 0:n], in0=t0[:, 0:n], in1=src, op=ALU.mult),
          reads=[r0, src_res], writes=[r0])
    kb.op("act", lambda e: e.activation(out=t1[:, 0:n], in_=t0[:, 0:n], func=AF.Sigmoid,
                                        scale=2.0 * math.sqrt(2.0 / math.pi)), reads=[r0], writes=[r1])
    kb.op("dve", lambda e: e.tensor_tensor(out=dst, in0=t1[:, 0:n], in1=src, op=ALU.mult),
          reads=[r1, src_res], writes=[dst_res])


class Rot:
    def __init__(s, kb, name, shape, dt, n=2):
        s.t = [kb.sb("%s%d" % (name, i), shape, dt) for i in range(n)]
        s.r = [Res() for _ in range(n)]
        s.i = 0
        s.n = n

    def get(s):
        i = s.i
        s.i = (i + 1) % s.n
        return s.t[i], s.r[i]


class PRot:
    def __init__(s, p, name, shape, dt, n=2):
        s.t = [p.psb("%s%d" % (name, i), shape, dt) for i in range(n)]
        s.r = [Res() for _ in range(n)]
        s.i = 0
        s.n = n

    def get(s):
        i = s.i
        s.i = (i + 1) % s.n
        return s.t[i], s.r[i]


class Prog:
    def __init__(s, cfg, stage):
        s.cfg = cfg
        s.stage = stage
        s.kb = KB(cfg)
        s.nc = s.kb.nc
        s.ins = []
        s.outs = []
        s.d = {}
        s.dres = {}
        s.ph = None
        s.uid = 0

    def phase_begin(s):
        import contextlib
        s.ph = contextlib.ExitStack()

    def phase_end(s):
        s.kb.barrier()
        s.ph.close()
        s.ph = None

    def psb(s, name, shape, dt):
        s.uid += 1
        return s.ph.enter_context(s.nc.sbuf_tensor("%s_%d" % (name, s.uid), list(shape), dt))

    def din(s, name, shape, dt=F32):
        if name not in s.d:
            s.d[name] = s.nc.dram_tensor(name, list(shape), dt, kind="ExternalInput").ap()
            s.dres[name] = Res()
            s.ins.append(name)
        return s.d[name]

    def dout(s, name, shape, dt=F32):
        s.d[name] = s.nc.dram_tensor(name, list(shape), dt, kind="ExternalOutput").ap()
        s.dres[name] = Res()
        s.outs.append(name)
        return s.d[name]

    def dint(s, name, shape, dt=F32):
        s.d[name] = s.nc.dram_tensor(name, list(shape), dt, kind="Internal").ap()
        s.dres[name] = Res()
        return s.d[name]

    def setup_common(s):
        kb, cfg = s.kb, s.cfg
        KC = cfg.KC
        s.csem = kb.sem("const")
        s.cres = Res()
        s.ones_bf = kb.sb("ones_bf", [128, 128], BF16)
        kb.op("dve", lambda e: e.memset(s.ones_bf[:], 1.0), writes=[s.cres])
        s.ident_bf = kb.sb("ident_bf", [128, 128], BF16)
        idd = s.din("ident", [128, 128], F32)
        kb.dma("pool", s.ident_bf[:], idd, s.csem, writes=[s.cres])
        s.wr = WRing(kb, 8)
        s.xsem = kb.sem("xld")
        s.ssem = kb.sem("xst")
        s.lsem = kb.sem("misc_ld")
        s.osem = kb.sem("misc_st")

    def load_mod(s):
        kb, cfg = s.kb, s.cfg
        KC, NMC, L = cfg.KC, cfg.NMC, cfg.DEPTH
        modg = s.d["modg"]
        s.modc = kb.sb("modc", [128, L, 8, NMC], F32)
        s.mres = Res()
        for l in range(L):
            kb.dma("sp", s.modc[:, l, :, :], modg.rearrange("(r p) (l c) -> p l r c", p=128, l=L)[:, l, :, :],
                   s.csem, reads=[s.dres["modg"]], writes=[s.mres])
        ng = s.din("normg", [128, L * 3 * KC], F32)
        s.normg = kb.sb("normg_sb", [128, L * 3 * KC], F32)
        kb.dma("sp", s.normg[:], ng, s.csem, writes=[s.mres])
        s.A = kb.sb("modA", [128, L * 3 * KC], F32)
        s.G = kb.sb("modG", [128, L * 3 * KC], F32)

        def mv(l, m):
            return s.modc[:, l, :, :].rearrange("p r c -> p (r c)")[:, m * KC:(m + 1) * KC]
        s.mv = mv
        for l in range(L):
            for j in range(3):
                o = (l * 3 + j) * KC
                kb.op("dve", lambda e, l=l, j=j, o=o: e.scalar_tensor_tensor(
                    out=s.A[:, o:o + KC], in0=mv(l, 3 * j + 1), scalar=1.0, in1=s.normg[:, o:o + KC],
                    op0=ALU.add, op1=ALU.mult), reads=[s.mres], writes=[s.mres])
                gm = 1.0 if j == 1 else 0.5
                kb.op("dve", lambda e, l=l, j=j, o=o, gm=gm: e.tensor_scalar(
                    out=s.G[:, o:o + KC], in0=mv(l, 3 * j + 2), scalar1=1.0, scalar2=gm,
                    op0=ALU.add, op1=ALU.mult), reads=[s.mres], writes=[s.mres])

    def colA(s, l, j, c):
        o = (l * 3 + j) * s.cfg.KC + c
        return s.A[:, o:o + 1]

    def colB(s, l, j, c):
        return s.mv(l, 3 * j)[:, c:c + 1]

    def colG(s, l, j, c):
        o = (l * 3 + j) * s.cfg.KC + c
        return s.G[:, o:o + 1]

    def emit_mod(s, out_ap, out_name):
        kb, cfg = s.kb, s.cfg
        KC, NMC, L = cfg.KC, cfg.NMC, cfg.DEPTH
        cl = s.din("c_l", [128, KC], F32)
        modw = s.din("modw_sh", [L * cfg.D, NMC * 128], F32)
        modb = s.din("modb_sh", [1, L * NMC * 128], F32)
        c_sb = kb.sb("c_sb", [128, KC], F32)
        cs_bf = kb.sb("cs_bf", [128, KC], BF16)
        mb_bf = kb.sb("mb_bf", [1, L * NMC * 128], BF16)
        one1 = kb.sb("one1", [1, 1], BF16)
        mo = kb.sb("mo_sb", [128, L * NMC], F32)
        r = Res()
        kb.dma("sp", c_sb[:], cl, s.csem, writes=[r])
        kb.dma("pool", mb_bf[:], modb, s.csem, writes=[r])
        kb.op("act", lambda e: e.activation(out=cs_bf[:], in_=c_sb[:], func=AF.Silu), reads=[r], writes=[r])
        kb.op("dve", lambda e: e.memset(one1[:], 1.0), writes=[r])
        bk = 0
        kb.op("dve", lambda e: e.memset(kb.banks[bk][:], 0.0), writes=[kb.bres[bk]])
        for l in range(L):
            for cb in range(0, NMC, 4):
                n = min(4, NMC - cb)
                for k in range(KC):
                    wt, wres = s.wr.load([(0, n * 128, modw[l * cfg.D + k * 128: l * cfg.D + (k + 1) * 128,
                                                            cb * 128:(cb + n) * 128])])
                    for i in range(n):
                        col = l * NMC + cb + i
                        kb.op("pe", lambda e, i=i, col=col, k=k: e.matmul(
                            kb.banks[bk][:, col:col + 1], wt[:, i * 128:(i + 1) * 128], cs_bf[:, k:k + 1],
                            start=False, stop=False, skip_group_check=True), reads=[wres, r], writes=[kb.bres[bk]])
                for i in range(n):
                    col = l * NMC + cb + i
                    kb.op("pe", lambda e, col=col: e.matmul(
                        kb.banks[bk][:, col:col + 1], mb_bf[0:1, col * 128:(col + 1) * 128], one1[0:1, 0:1],
                        start=False, stop=True, skip_group_check=True), reads=[r], writes=[kb.bres[bk]])
        kb.op("dve", lambda e: e.tensor_copy(out=mo[:], in_=kb.banks[bk][:, 0:L * NMC]),
              reads=[kb.bres[bk]], writes=[r])
        kb.dma("sp", out_ap, mo[:], s.csem, reads=[r], writes=[s.dres[out_name]])

    def seg_alloc(s):
        kb, cfg = s.kb, s.cfg
        KC, FC, T = cfg.KC, cfg.FC, cfg.T
        s.xg = s.psb("xg", [128, KC, T], F32)
        s.xres = [Res() for _ in range(KC)]
        s.hT = s.psb("hT", [128, max(KC, 16), T], BF16)
        s.hres = [Res() for _ in range(max(KC, 16))]
        s.aT = s.psb("aT", [128, FC, T], BF16)
        s.ares = [Res() for _ in range(FC)]
        s.rstd = s.psb("rstd", [128, T], F32)
        s.rstd_res = Res()
        s.sq = PRot(s, "sq", [128, T], BF16, 2)
        s.tmpf = PRot(s, "tmpf", [128, 512], F32, 4)
        if not hasattr(s, "xsem"):
            s.xsem = kb.sem("xld")
            s.ssem = kb.sem("xst")
            s.lsem = kb.sem("misc_ld")
            s.osem = kb.sem("misc_st")

    def load_x(s, x_ap, x_name, g):
        kb, cfg = s.kb, s.cfg
        T = cfg.T
        src = x_ap.rearrange("(c p) t -> p c t", p=128)[:, :, g * T:(g + 1) * T]
        kb.dma("sp", s.xg[:], src, s.xsem, reads=[s.dres[x_name]], writes=s.xres)

    def store_x(s, x_ap, x_name, g):
        kb, cfg = s.kb, s.cfg
        T = cfg.T
        dst = x_ap.rearrange("(c p) t -> p c t", p=128)[:, :, g * T:(g + 1) * T]
        kb.dma("sp", dst, s.xg[:], s.ssem, reads=s.xres, writes=[s.dres[x_name]])

    def rms_stats(s):
        kb, cfg = s.kb, s.cfg
        KC, T = cfg.KC, cfg.T
        bs = kb.next_banks()
        b = bs[0]
        for c in range(KC):
            sq, sqr = s.sq.get()
            kb.op("act", lambda e, c=c, sq=sq: e.activation(out=sq[:], in_=s.xg[:, c, :], func=AF.Square),
                  reads=[s.xres[c]], writes=[sqr])
            kb.op("pe", lambda e, c=c, sq=sq: e.matmul(kb.banks[b][:, 0:T], s.ones_bf[:], sq[:],
                                                       start=(c == 0), stop=(c == KC - 1)),
                  reads=[sqr, s.cres], writes=[kb.bres[b]])
        kb.op("dve", lambda e: e.tensor_scalar(out=s.rstd[:], in0=kb.banks[b][:, 0:T], scalar1=1.0 / cfg.D,
                                               scalar2=EPS, op0=ALU.mult, op1=ALU.add),
              reads=[kb.bres[b]], writes=[s.rstd_res])
        kb.op("act", lambda e: e.activation(out=s.rstd[:], in_=s.rstd[:], func=AF.Sqrt),
              reads=[s.rstd_res], writes=[s.rstd_res])
        kb.op("dve", lambda e: e.reciprocal(out=s.rstd[:], in_=s.rstd[:]), reads=[s.rstd_res], writes=[s.rstd_res])

    def norm_mod(s, l, j):
        kb, cfg = s.kb, s.cfg
        KC, T = cfg.KC, cfg.T
        s.rms_stats()
        for c in range(KC):
            tp, tr = s.tmpf.get()
            kb.op("dve", lambda e, c=c, tp=tp: e.tensor_tensor(out=tp[:, 0:T], in0=s.xg[:, c, :], in1=s.rstd[:],
                                                               op=ALU.mult),
                  reads=[s.xres[c], s.rstd_res], writes=[tr])
            kb.op("act", lambda e, c=c, tp=tp: e.activation(out=s.hT[:, c, :], in_=tp[:, 0:T], func=AF.Identity,
                                                            scale=s.colA(l, j, c), bias=s.colB(l, j, c)),
                  reads=[tr, s.mres], writes=[s.hres[c]])

    def ffn(s, l, j, sub):
        kb, cfg = s.kb, s.cfg
        KC, FC, T, D, DFF = cfg.KC, cfg.FC, cfg.T, cfg.D, cfg.DFF
        w1 = s.din("w1_%d_%d" % (l, j), [D, 2 * DFF])
        w2 = s.din("w2_%d_%d" % (l, j), [DFF, D])
        s.norm_mod(l, sub)
        for fb in range(0, FC, 2):
            nf = min(2, FC - fb)

            def pieces(k, fb=fb, nf=nf):
                rows = slice(k * 128, (k + 1) * 128)
                return [(0, nf * 128, w1[rows, fb * 128:(fb + nf) * 128]),
                        (256, nf * 128, w1[rows, DFF + fb * 128:DFF + (fb + nf) * 128])]

            def evac(bs, fb=fb, nf=nf):
                for i in range(nf):
                    tp, tr = s.tmpf.get()
                    bg, bu = bs[i], bs[2 + i]
                    kb.op("act", lambda e, tp=tp, bg=bg: e.activation(out=tp[:, 0:T], in_=kb.banks[bg][:, 0:T],
                                                                      func=AF.Silu),
                          reads=[kb.bres[bg]], writes=[tr])
                    kb.op("dve", lambda e, tp=tp, bu=bu, i=i: e.tensor_tensor(
                        out=s.aT[:, fb + i, :], in0=tp[:, 0:T], in1=kb.banks[bu][:, 0:T], op=ALU.mult),
                        reads=[tr, kb.bres[bu]], writes=[s.ares[fb + i]])

            def mm(bs, nf=nf):
                pass
            bs = kb.next_banks()
            for k in range(KC):
                wt, wres = s.wr.load(pieces(k))
                for i in range(nf):
                    for (b, c0) in ((bs[i], i * 128), (bs[2 + i], 256 + i * 128)):
                        kb.op("pe", lambda e, b=b, c0=c0, k=k, wt=wt: e.matmul(
                            kb.banks[b][:, 0:T], wt[:, c0:c0 + 128], s.hT[:, k, :],
                            start=(k == 0), stop=(k == KC - 1)),
                            reads=[wres, s.hres[k]], writes=[kb.bres[b]])
            evac(bs)
        for seg in range(0, KC, 4):
            nch = min(4, KC - seg)

            def pieces2(k, seg=seg, nch=nch):
                return [(0, nch * 128, w2[k * 128:(k + 1) * 128, seg * 128:(seg + nch) * 128])]

            def evac2(bs, seg=seg, nch=nch):
                for i in range(nch):
                    c = seg + i
                    b = bs[i]
                    kb.op("dve", lambda e, c=c, b=b: e.scalar_tensor_tensor(
                        out=s.xg[:, c, :], in0=kb.banks[b][:, 0:T], scalar=s.colG(l, sub, c), in1=s.xg[:, c, :],
                        op0=ALU.mult, op1=ALU.add), reads=[kb.bres[b], s.xres[c], s.mres], writes=[s.xres[c]])
            linear(kb, s.wr, s.aT, s.ares, FC, T, pieces2, "fm", nch, evac2)

    def final_norm(s):
        kb, cfg = s.kb, s.cfg
        KC, T = cfg.KC, cfg.T
        fg = s.din("finalg", [128, KC], F32)
        if getattr(s, "fg_ph", None) is not s.ph:
            s.fg_ph = s.ph
            s.fg_sb = s.psb("fg_sb", [128, KC], F32)
            kb.dma("sp", s.fg_sb[:], fg, s.csem, writes=[s.mres])
        s.rms_stats()
        for c in range(KC):
            kb.op("dve", lambda e, c=c: e.scalar_tensor_tensor(
                out=s.xg[:, c, :], in0=s.xg[:, c, :], scalar=s.fg_sb[:, c:c + 1], in1=s.rstd[:],
                op0=ALU.mult, op1=ALU.mult), reads=[s.xres[c], s.rstd_res, s.mres], writes=[s.xres[c]])


def col_layout(v):
    v = np.asarray(v, np.float32)
    n = v.shape[-1] // 128
    return np.ascontiguousarray(v.reshape(-1, 128).T)


def shard_tokens_T(x2d, cfg):
    S, D = x2d.shape
    xb = x2d.reshape(cfg.NB, 8, 128, D)
    out = []
    for c in range(8):
        xc = xb[:, c].reshape(cfg.NT, D)
        out.append(np.ascontiguousarray(xc.T))
    return out


def unshard_tokens_T(outs, cfg):
    D = outs[0].shape[0]
    full = np.empty((cfg.NB, 8, 128, D), np.float32)
    for c in range(8):
        full[:, c] = outs[c].T.reshape(cfg.NB, 128, D)
    return full.reshape(cfg.S, D)


def run_prog(p, maps):
    in_maps = [{k: m[k] for k in p.ins} for m in maps]
    res = run_bass_kernel_spmd(p.nc, in_maps, core_ids=list(range(8)))
    return res.results


def build_mod_prog(cfg):
    p = Prog(cfg, "mod")
    p.setup_common()
    out = p.dout("modx", [128, cfg.DEPTH * cfg.NMC])
    p.emit_mod(out, "modx")
    p.kb.finish()
    return p


def mod_inputs(cfg, c, mod_w, mod_b):
    L, D, NMC = cfg.DEPTH, cfg.D, cfg.NMC
    w = NMC * 128
    maps = []
    c_l = col_layout(np.asarray(c).reshape(D))
    ident = np.eye(128, dtype=np.float32)
    for r in range(8):
        maps.append({
            "c_l": c_l, "ident": ident,
            "modw_sh": np.ascontiguousarray(np.asarray(mod_w)[:, :, r * w:(r + 1) * w]).reshape(L * D, w),
            "modb_sh": np.ascontiguousarray(np.asarray(mod_b)[:, r * w:(r + 1) * w]).reshape(1, L * w),
        })
    return maps


def _ab_methods():
    def ab_alloc(s, l):
        kb, cfg = s.kb, s.cfg
        T, TB, D = cfg.T, cfg.TB, cfg.D
        s.qT_sb = s.psb("qT_sb", [128, 8, T], BF16)
        s.kT_sb = s.psb("kT_sb", [128, 8, T], BF16)
        s.v_sb = s.psb("v_sb", [128, TB, 1024], BF16)
        s.uT_sb = s.psb("uT_sb", [128, 8, T], BF16)
        s.z_sb = s.psb("z_sb", [128, TB, 1024], F32)
        s.zn_sb = s.psb("zn_sb", [128, TB, 1024], BF16)
        s.ybT_sb = s.psb("ybT_sb", [128, 8, T], BF16)
        s.big = PRot(s, "bigf", [128, 1024], F32, 2)
        s.st = s.psb("lnst", [128, 8], F32)
        s.lngb = s.psb("lngb", [128, 2, 1024], F32)
        s.wmT = s.psb("wmT", [128, 8, 128], BF16)
        s.wmf = s.psb("wmf", [128, 8, 128], F32)
        s.trim = s.psb("trim", [128, 128], F32)
        s.bsrow = s.psb("bsrow", [1, 1024], BF16)
        s.abres = {k: Res() for k in ("q", "k", "v", "u", "z", "zn", "yb", "st", "c")}
        sgp = s.din("sgp_%d" % l, [1, 3 * 1024])
        sgw = s.din("sgwT_%d" % l, [128, 8 * 128])
        trim = s.din("trimask", [128, 128])
        r = s.abres["c"]
        for i in range(2):
            kb.dma("sp", s.lngb[:, i, :], sgp[0, i * 1024:(i + 1) * 1024].partition_broadcast(128), s.csem, writes=[r])
        kb.dma("pool", s.bsrow[:], sgp[:, 2048:3072], s.csem, writes=[r])
        kb.dma("sp", s.wmf[:].rearrange("p g t -> p (g t)"), sgw, s.csem, writes=[r])
        kb.dma("sp", s.trim[:], trim, s.csem, writes=[r])
        for g8 in range(8):
            kb.op("dve", lambda e, g8=g8: e.tensor_tensor(out=s.wmT[:, g8, :], in0=s.wmf[:, g8, :], in1=s.trim[:],
                                                          op=ALU.mult), reads=[r], writes=[r])

    def ab_inproj(s, l, g):
        kb, cfg = s.kb, s.cfg
        T, TB, D, KC = cfg.T, cfg.TB, cfg.D, cfg.KC
        win = s.din("win_%d" % l, [D, 5120])
        R = s.abres

        def pc(c0, n):
            return lambda k: [(0, n, win[k * 128:(k + 1) * 128, c0:c0 + n])]

        def copy_evac(dst, res, scale):
            def f(bs):
                for i in range(4):
                    b = bs[i]
                    if kb.ev() == "act":
                        kb.op("act", lambda e, i=i, b=b: e.activation(out=dst(i), in_=kb.banks[b][:, 0:T],
                                                                      func=AF.Identity, scale=scale),
                              reads=[kb.bres[b]], writes=[res])
                    else:
                        kb.op("dve", lambda e, i=i, b=b: e.tensor_scalar(out=dst(i), in0=kb.banks[b][:, 0:T],
                                                                         scalar1=scale, scalar2=None, op0=ALU.mult),
                              reads=[kb.bres[b]], writes=[res])
            return f
        for sg in range(2):
            linear(kb, s.wr, s.hT, s.hres, KC, T, pc(sg * 512, 512), "fm", 4,
                   copy_evac(lambda i, sg=sg: s.qT_sb[:, sg * 4 + i, :], R["q"], 0.125))
        for sg in range(2):
            linear(kb, s.wr, s.hT, s.hres, KC, T, pc(1024 + sg * 512, 512), "fm", 4,
                   copy_evac(lambda i, sg=sg: s.kT_sb[:, sg * 4 + i, :], R["k"], 1.0))
        for sg in range(2):
            def evv(bs, sg=sg):
                for tb in range(TB):
                    b = bs[tb]
                    kb.op(kb.ev(), lambda e, tb=tb, b=b: e.tensor_copy(out=s.v_sb[:, tb, sg * 512:(sg + 1) * 512],
                                                                       in_=kb.banks[b][:, 0:512])
                          if e is kb.q["dve"].eng else e.activation(out=s.v_sb[:, tb, sg * 512:(sg + 1) * 512],
                                                                    in_=kb.banks[b][:, 0:512], func=AF.Identity),
                          reads=[kb.bres[b]], writes=[R["v"]])
            linear(kb, s.wr, s.hT, s.hres, KC, T, pc(2048 + sg * 512, 512), "tm", 512, evv)
        for sg in range(2):
            def evu(bs, sg=sg):
                for i in range(4):
                    b = bs[i]
                    t0, r0 = s.tmpf.get()
                    t1, r1 = s.tmpf.get()
                    gelu_tanh(kb, kb.banks[b][:, 0:T], kb.bres[b], s.uT_sb[:, sg * 4 + i, :], R["u"], T,
                              (t0, t1), (r0, r1))
            linear(kb, s.wr, s.hT, s.hres, KC, T, pc(3072 + sg * 512, 512), "fm", 4, evu)
        for sg in range(2):
            def evz(bs, sg=sg):
                for tb in range(TB):
                    b = bs[tb]
                    t0, r0 = s.tmpf.get()
                    t1, r1 = s.tmpf.get()
                    gelu_tanh(kb, kb.banks[b][:, 0:512], kb.bres[b], s.z_sb[:, tb, sg * 512:(sg + 1) * 512], R["z"], 512,
                              (t0, t1), (r0, r1))
            linear(kb, s.wr, s.hT, s.hres, KC, T, pc(4096 + sg * 512, 512), "tm", 512, evz)
        st, rs = s.st, R["st"]
        for tb in range(TB):
            z = s.z_sb[:, tb, :]
            bt, br = s.big.get()
            kb.op("dve", lambda e, z=z: e.reduce_sum(out=st[:, 0:1], in_=z, axis=AX.X), reads=[R["z"]], writes=[rs])
            kb.op("act", lambda e, z=z, bt=bt: e.activation(out=bt[:], in_=z, func=AF.Square, accum_out=st[:, 1:2]),
                  reads=[R["z"]], writes=[rs, br])
            kb.op("dve", lambda e: e.tensor_scalar(out=st[:, 2:4], in0=st[:, 0:2], scalar1=1.0 / 1024, scalar2=None,
                                                   op0=ALU.mult), reads=[rs], writes=[rs])
            kb.op("dve", lambda e: e.tensor_tensor(out=st[:, 4:5], in0=st[:, 2:3], in1=st[:, 2:3], op=ALU.mult),
                  reads=[rs], writes=[rs])
            kb.op("dve", lambda e: e.tensor_tensor(out=st[:, 4:5], in0=st[:, 3:4], in1=st[:, 4:5], op=ALU.subtract),
                  reads=[rs], writes=[rs])
            kb.op("dve", lambda e: e.tensor_scalar(out=st[:, 4:5], in0=st[:, 4:5], scalar1=EPS, scalar2=None,
                                                   op0=ALU.add), reads=[rs], writes=[rs])
            kb.op("act", lambda e: e.activation(out=st[:, 4:5], in_=st[:, 4:5], func=AF.Sqrt), reads=[rs], writes=[rs])
            kb.op("dve", lambda e: e.reciprocal(out=st[:, 5:6], in_=st[:, 4:5]), reads=[rs], writes=[rs])
            kb.op("dve", lambda e: e.scalar_tensor_tensor(out=st[:, 6:7], in0=st[:, 2:3], scalar=-1.0, in1=st[:, 5:6],
                                                          op0=ALU.mult, op1=ALU.mult), reads=[rs], writes=[rs])
            kb.op("act", lambda e, z=z, bt=bt: e.activation(out=bt[:], in_=z, func=AF.Identity, scale=st[:, 5:6],
                                                            bias=st[:, 6:7]), reads=[R["z"], rs], writes=[br])
            kb.op("dve", lambda e, bt=bt: e.tensor_tensor(out=bt[:], in0=bt[:], in1=s.lngb[:, 0, :], op=ALU.mult),
                  reads=[br, R["c"]], writes=[br])
            kb.op("dve", lambda e, bt=bt, tb=tb: e.tensor_tensor(out=s.zn_sb[:, tb, :], in0=bt[:], in1=s.lngb[:, 1, :],
                                                                 op=ALU.add), reads=[br, R["c"]], writes=[R["zn"]])
        for g8 in range(8):
            b = kb.next_banks()[0]
            for tb in range(TB):
                kb.op("pe", lambda e, tb=tb, b=b: e.matmul(kb.banks[b][:, tb * 128:(tb + 1) * 128],
                                                           s.zn_sb[:, tb, g8 * 128:(g8 + 1) * 128], s.wmT[:, g8, :],
                                                           start=True, stop=False),
                      reads=[R["zn"], R["c"]], writes=[kb.bres[b]])
                kb.op("pe", lambda e, tb=tb, b=b: e.matmul(kb.banks[b][:, tb * 128:(tb + 1) * 128],
                                                           s.ones_bf[0:1, 0:128], s.bsrow[0:1, g8 * 128:(g8 + 1) * 128],
                                                           start=False, stop=True),
                      reads=[R["c"], s.cres], writes=[kb.bres[b]])
            kb.op("dve", lambda e, b=b: e.tensor_tensor(out=s.ybT_sb[:, g8, :], in0=kb.banks[b][:, 0:T],
                                                        in1=s.uT_sb[:, g8, :], op=ALU.mult),
                  reads=[kb.bres[b], R["u"]], writes=[R["yb"]])
        tok = slice(g * T, (g + 1) * T)
        for nm, sbt, rk in (("qT", s.qT_sb, "q"), ("kx", s.kT_sb, "k"), ("ybx", s.ybT_sb, "yb")):
            kb.dma("sp", s.d[nm].rearrange("(c p) t -> p c t", p=128)[:, :, tok], sbt[:], s.osem,
                   reads=[R[rk]], writes=[s.dres[nm]])
        kb.dma("sp", s.d["vx"].rearrange("(tb p) n -> p tb n", p=128)[:, g * TB:(g + 1) * TB, :], s.v_sb[:], s.osem,
               reads=[R["v"]], writes=[s.dres["vx"]])

    Prog.ab_alloc = ab_alloc
    Prog.ab_inproj = ab_inproj


_ab_methods()


def _attn_methods():
    def ab_attn(s, l):
        kb, cfg = s.kb, s.cfg
        NT, NB = cfg.NT, cfg.NB
        lam_init = 0.8 - 0.6 * math.exp(-0.3 * l)
        qT, kg, vg, yaT = s.d["qT"], s.d["kg"], s.d["vg"], s.d["yaT"]
        abias = s.din("abias", [128, 16 * 9 * 128])
        b31d = s.din("b31", [128, 16])
        lamd = s.din("lam_%d" % l, [1, 256])
        subg = s.din("subg_%d" % l, [1, 128])
        s.phase_begin()
        kT = s.psb("kT_all", [128, 8, NT], BF16)
        va = s.psb("va", [128, 8 * NB, 130], BF16)
        qh = s.psb("qh", [128, NT], BF16)
        bt = s.psb("bt", [128, 2 * 9 * 128], BF16)
        ya = s.psb("ya_sb", [128, 8, NT], BF16)
        b31 = s.psb("b31_sb", [128, 16], F32)
        lam_sb = s.psb("lam_sb", [128, 256], F32)
        lst = s.psb("lam_st", [128, 8], F32)
        sgb = s.psb("sgb", [128, 128], F32)
        pT = [PRot(s, "pT%d" % m, [128, 512], BF16, 2) for m in range(2)]
        of = PRot(s, "of", [128, 128], F32, 2)
        ob = PRot(s, "ob", [128, 128], BF16, 2)
        fst = s.psb("fin_st", [128, 8], F32)
        junk = s.psb("junk", [128, 128], F32)
        if not hasattr(s, "asem"):
            s.asem = [kb.sem("att%d" % i) for i in range(4)]
        rk, rv, rq, rb, rc, rya, rf = Res(), Res(), Res(), Res(), Res(), Res(), Res()
        kb.dma("sp", b31[:], b31d, s.csem, writes=[rc])
        kb.dma("sp", lam_sb[:], lamd[0, :].partition_broadcast(128), s.csem, writes=[rc])
        kb.dma("sp", sgb[:], subg[0, :].partition_broadcast(128), s.csem, writes=[rc])
        kb.op("dve", lambda e: e.memset(va[:, :, 128:130], 1.0), writes=[rv])
        kb.op("dve", lambda e: e.tensor_scalar(out=sgb[:], in0=sgb[:], scalar1=1.0 - lam_init, scalar2=None,
                                               op0=ALU.mult), reads=[rc], writes=[rc])
        for i in range(2):
            kb.op("dve", lambda e, i=i: e.tensor_tensor(out=lam_sb[:, i * 128:i * 128 + 64],
                                                        in0=lam_sb[:, i * 128:i * 128 + 64],
                                                        in1=lam_sb[:, i * 128 + 64:i * 128 + 128], op=ALU.mult),
                  reads=[rc], writes=[rc])
            kb.op("dve", lambda e, i=i: e.reduce_sum(out=lst[:, i:i + 1], in_=lam_sb[:, i * 128:i * 128 + 64], axis=AX.X),
                  reads=[rc], writes=[rc])
        kb.op("act", lambda e: e.activation(out=lst[:, 0:2], in_=lst[:, 0:2], func=AF.Exp), reads=[rc], writes=[rc])
        kb.op("dve", lambda e: e.scalar_tensor_tensor(out=lst[:, 2:3], in0=lst[:, 1:2], scalar=-lam_init, in1=lst[:, 0:1],
                                                      op0=ALU.add, op1=ALU.subtract), reads=[rc], writes=[rc])
        neglam = lst[:, 2:3]
        SB = [0, 1, 2, 3]
        flip = 0
        for h in range(8):
            kb.dma("sp", kT[:], kg.rearrange("(r f) t -> f r t", r=8)[h * 128:(h + 1) * 128, :, :], s.asem[0],
                   reads=[s.dres["kg"]], writes=[rk])
            kb.dma("sp", va[:, :, 0:128], vg.rearrange("(kb p) n -> p kb n", p=128)[:, :, h * 128:(h + 1) * 128],
                   s.asem[1], reads=[s.dres["vg"]], writes=[rv])
            kb.dma("sp", qh[:], qT[h * 128:(h + 1) * 128, :], s.asem[2], reads=[s.dres["qT"]], writes=[rq])
            kb.dma("pool", bt[:], abias[:, h * 2 * 9 * 128:(h + 1) * 2 * 9 * 128], s.asem[3], writes=[rb])
            for j in range(NB):
                normal = [(r, jp) for jp in range(j) for r in range(8) if not (jp == j - 1 and r == 7)]
                special = ([(7, j - 1, 0)] if j >= 1 else []) + [(r, j, 1 + r) for r in range(8)]
                chunks = [("n", normal[i:i + 4]) for i in range(0, len(normal), 4)]
                chunks += [("s", special[i:i + 4]) for i in range(0, len(special), 4)]
                bO = [4 + 2 * (flip & 1), 5 + 2 * (flip & 1)]
                flip += 1
                nblk = len(normal) + len(special)
                done = 0
                for kind, blks in chunks:
                    n = len(blks)
                    pts = []
                    for m in range(2):
                        b = SB[(2 * (done // 4) + m) % 4] if False else SB[m + 2 * ((done // 4) % 2)]
                        hm = 2 * h + m
                        ps = slice(m * 64, (m + 1) * 64)
                        if kind == "s":
                            p0 = blks[0][2]
                            kb.op("pe", lambda e, b=b, m=m, p0=p0, n=n: e.matmul(
                                kb.banks[b][:, 0:n * 128], s.ident_bf[:],
                                bt[:, (m * 9 + p0) * 128:(m * 9 + p0 + n) * 128], start=True, stop=False),
                                reads=[rb, s.cres], writes=[kb.bres[b]])
                        for i, blk in enumerate(blks):
                            r, jp = blk[0], blk[1]
                            kb.op("pe", lambda e, b=b, i=i, r=r, jp=jp, ps=ps: e.matmul(
                                kb.banks[b][:, i * 128:(i + 1) * 128], kT[ps, r, jp * 128:(jp + 1) * 128],
                                qh[ps, j * 128:(j + 1) * 128], start=(kind == "n"), stop=True,
                                skip_group_check=True),
                                reads=[rk, rq], writes=[kb.bres[b]])
                        pt, pr = pT[m].get()
                        if kind == "n":
                            kb.op("act", lambda e, b=b, pt=pt, hm=hm, n=n: e.activation(
                                out=pt[:, 0:n * 128], in_=kb.banks[b][:, 0:n * 128], func=AF.Exp,
                                bias=b31[:, hm:hm + 1]), reads=[kb.bres[b], rc], writes=[pr])
                        else:
                            kb.op("act", lambda e, b=b, pt=pt, n=n: e.activation(
                                out=pt[:, 0:n * 128], in_=kb.banks[b][:, 0:n * 128], func=AF.Exp),
                                reads=[kb.bres[b]], writes=[pr])
                        pts.append((pt, pr))
                    for m in range(2):
                        pt, pr = pts[m]
                        for i, blk in enumerate(blks):
                            kbi = blk[0] * NB + blk[1]
                            first = (done + i == 0)
                            last = (done + i == nblk - 1)
                            kb.op("pe", lambda e, m=m, i=i, kbi=kbi, pt=pt, first=first, last=last: e.matmul(
                                kb.banks[bO[m]][:, 0:130], pt[:, i * 128:(i + 1) * 128], va[:, kbi, :],
                                start=first, stop=last), reads=[pr, rv], writes=[kb.bres[bO[m]]])
                    done += n
                O0, O1 = kb.banks[bO[0]], kb.banks[bO[1]]
                r0, r1 = kb.bres[bO[0]], kb.bres[bO[1]]
                kb.op("dve", lambda e: e.reciprocal(out=fst[:, 0:1], in_=O0[:, 128:129]), reads=[r0], writes=[rf])
                kb.op("dve", lambda e: e.reciprocal(out=fst[:, 1:2], in_=O1[:, 128:129]), reads=[r1], writes=[rf])
                o_t, o_r = of.get()
                kb.op("dve", lambda e, o_t=o_t: e.tensor_scalar(out=o_t[:], in0=O1[:, 0:128], scalar1=fst[:, 1:2],
                                                                scalar2=neglam, op0=ALU.mult, op1=ALU.mult),
                      reads=[r1, rf, rc], writes=[o_r])
                kb.op("dve", lambda e, o_t=o_t: e.scalar_tensor_tensor(out=o_t[:], in0=O0[:, 0:128], scalar=fst[:, 0:1],
                                                                       in1=o_t[:], op0=ALU.mult, op1=ALU.add),
                      reads=[r0, rf, o_r], writes=[o_r])
                kb.op("act", lambda e, o_t=o_t: e.activation(out=junk[:], in_=o_t[:], func=AF.Square,
                                                             accum_out=fst[:, 2:3]), reads=[o_r], writes=[rf])
                kb.op("dve", lambda e: e.tensor_scalar(out=fst[:, 3:4], in0=fst[:, 2:3], scalar1=1.0 / 128, scalar2=EPS,
                                                       op0=ALU.mult, op1=ALU.add), reads=[rf], writes=[rf])
                kb.op("act", lambda e: e.activation(out=fst[:, 3:4], in_=fst[:, 3:4], func=AF.Sqrt), reads=[rf], writes=[rf])
                kb.op("dve", lambda e: e.reciprocal(out=fst[:, 4:5], in_=fst[:, 3:4]), reads=[rf], writes=[rf])
                ob_t, ob_r = ob.get()
                kb.op("dve", lambda e, o_t=o_t, ob_t=ob_t: e.scalar_tensor_tensor(
                    out=ob_t[:], in0=o_t[:], scalar=fst[:, 4:5], in1=sgb[:], op0=ALU.mult, op1=ALU.mult),
                    reads=[o_r, rf, rc], writes=[ob_r])
                tb_ = SB[(flip) % 4]
                tps = kb.banks[tb_][:].bitcast(BF16)
                kb.op("pe", lambda e, ob_t=ob_t, tps=tps: e.transpose(tps[:, 0:128], ob_t[:], s.ident_bf[:]),
                      reads=[ob_r, s.cres], writes=[kb.bres[tb_]])
                kb.op("act", lambda e, tps=tps, h=h: e.activation(out=ya[:, h, j * 128:(j + 1) * 128], in_=tps[:, 0:128],
                                                                  func=AF.Identity), reads=[kb.bres[tb_]], writes=[rya])
        kb.dma("sp", yaT.rearrange("(c p) t -> p c t", p=128), ya[:], s.osem, reads=[rya], writes=[s.dres["yaT"]])
        s.phase_end()

    def out_proj(s, l, g, nyb):
        kb, cfg = s.kb, s.cfg
        T, D, KC = cfg.T, cfg.D, cfg.KC
        wout = s.din("wout_%d" % l, [2048, D])
        tok = slice(g * T, (g + 1) * T)
        nya = 16 - nyb
        kb.dma("sp", s.hT[:, 0:nya, :], s.d["yaT"].rearrange("(c p) t -> p c t", p=128)[:, :, tok], s.lsem,
               reads=[s.dres["yaT"]], writes=s.hres[0:nya])
        if nyb:
            kb.dma("sp", s.hT[:, nya:16, :], s.d["ybx"].rearrange("(c p) t -> p c t", p=128)[:, :, tok], s.lsem,
                   reads=[s.dres["ybx"]], writes=s.hres[nya:16])
        for seg in range(0, KC, 4):
            nch = min(4, KC - seg)

            def pieces(k, seg=seg, nch=nch):
                return [(0, nch * 128, wout[k * 128:(k + 1) * 128, seg * 128:(seg + nch) * 128])]

            def evac(bs, seg=seg, nch=nch):
                for i in range(nch):
                    c = seg + i
                    b = bs[i]
                    kb.op("dve", lambda e, c=c, b=b: e.scalar_tensor_tensor(
                        out=s.xg[:, c, :], in0=kb.banks[b][:, 0:T], scalar=s.colG(l, 1, c), in1=s.xg[:, c, :],
                        op0=ALU.mult, op1=ALU.add), reads=[kb.bres[b], s.xres[c], s.mres], writes=[s.xres[c]])
            linear(kb, s.wr, s.hT, s.hres, 16, T, pieces, "fm", nch, evac)

    Prog.ab_attn = ab_attn
    Prog.out_proj = out_proj


_attn_methods()


def rel_bucket_np(dist):
    n = np.maximum(dist, 0)
    nf = np.maximum(n, 1).astype(np.float32)
    large = 16 + (np.log(nf / np.float32(16)) / np.float32(math.log(8)) * 16).astype(np.int32)
    large = np.minimum(large, 31)
    return np.where(n < 16, n, large)


def special_bucket_idx(c):
    k = np.arange(128)[:, None]
    q = np.arange(128)[None, :]
    idx = np.empty((9, 128, 128), np.int64)
    for pos in range(9):
        db = (c + 1) if pos == 0 else (c - (pos - 1))
        if db < 0:
            idx[pos] = 32
        else:
            dist = db * 128 + q - k
            idx[pos] = np.where(dist >= 0, rel_bucket_np(dist), 32)
    return idx


def build_stage(cfg, k):
    L = cfg.DEPTH
    p = Prog(cfg, k)
    p.setup_common()
    p.din("modg", [8 * 128, L * cfg.NMC])
    xin = p.din("xT_in", [cfg.D, cfg.NT])
    xo = p.dout("xT_out", [cfg.D, cfg.NT])
    p.load_mod()
    NT = cfg.NT
    l_out = k - 1 if k >= 1 else None
    l_in = k if k < L else None
    if l_out is not None:
        if l_out % 2 == 0:
            p.din("qT", [1024, NT], BF16)
            p.din("kg", [8 * 1024, NT], BF16)
            p.din("vg", [8 * NT, 1024], BF16)
            p.din("ybx", [1024, NT], BF16)
            p.dint("yaT", [1024, NT], BF16)
            p.ab_attn(l_out)
        else:
            p.dsa_declare_in()
            p.dint("yaT", [2048, NT], BF16)
            p.dsa_attn(l_out)
    if l_in is not None:
        if l_in % 2 == 0:
            for nm, shp in (("qT", [1024, NT]), ("kx", [1024, NT]), ("ybx", [1024, NT]), ("vx", [NT, 1024])):
                if nm in p.d:
                    nm2 = nm
                p.dout(nm if nm not in p.d else nm + "_o", shp, BF16)
        else:
            p.dsa_declare_out()
    p.phase_begin()
    p.seg_alloc()
    if l_in is not None:
        if l_in % 2 == 0:
            p.ab_alloc(l_in)
        else:
            p.dsa_alloc(l_in)
    p.min_sbuf = min(getattr(p, "min_sbuf", 1 << 30), p.nc.sbuf_bytes_remaining)
    for g in range(cfg.NG):
        p.load_x(xin, "xT_in", g)
        if l_out is not None:
            p.out_proj(l_out, g, 8 if l_out % 2 == 0 else 0)
            p.ffn(l_out, 1, 2)
        if l_in is not None:
            p.ffn(l_in, 0, 0)
            p.norm_mod(l_in, 1)
            if l_in % 2 == 0:
                p.ab_inproj(l_in, g)
            else:
                p.dsa_inproj(l_in, g)
        else:
            p.final_norm()
        p.store_x(xo, "xT_out", g)
    p.phase_end()
    p.kb.finish()
    return p


def kernel(x, c, norm_g, mod_w, mod_b, ffn_w1, ffn_w2, rel_table, ab_w_in, ab_w_out, diff_lam, diff_subln_g,
           sg_ln_g, sg_ln_b, sg_w, sg_b, dsa_w_in, dsa_w_out, final_g, _cfg=None):
    A = lambda v: np.asarray(v, np.float32)
    x = A(x)
    cfg = _cfg or Cfg(S=x.shape[1], D=x.shape[2], DFF=np.asarray(ffn_w2).shape[2], DEPTH=np.asarray(norm_g).shape[0])
    L = cfg.DEPTH
    ident = np.eye(128, dtype=np.float32)
    pm = build_mod_prog(cfg)
    res = run_prog(pm, mod_inputs(cfg, c, A(mod_w), A(mod_b)))
    modg = np.concatenate([r["modx"] for r in res], 0)
    rel_table = A(rel_table)
    ext = np.concatenate([rel_table, np.full((1, 16), NEG, np.float32)], 0)
    common = {"modg": modg, "ident": ident, "normg": col_layout(A(norm_g)), "finalg": col_layout(A(final_g)),
              "trimask": np.triu(np.ones((128, 128), np.float32)),
              "b31": np.ascontiguousarray(np.broadcast_to(rel_table[31][None, :], (128, 16)))}
    for l in range(L):
        for j in range(2):
            common["w1_%d_%d" % (l, j)] = A(ffn_w1)[l, j]
            common["w2_%d_%d" % (l, j)] = A(ffn_w2)[l, j]
        jj = l // 2
        if l % 2 == 0:
            common["win_%d" % l] = A(ab_w_in)[jj]
            common["wout_%d" % l] = A(ab_w_out)[jj]
            common["sgp_%d" % l] = np.concatenate([A(sg_ln_g)[jj], A(sg_ln_b)[jj], A(sg_b)[jj].reshape(-1)])[None]
            common["sgwT_%d" % l] = np.ascontiguousarray(A(sg_w)[jj].transpose(2, 0, 1)).reshape(128, 1024)
            common["lam_%d" % l] = A(diff_lam)[jj].reshape(1, 256)
            common["subg_%d" % l] = A(diff_subln_g)[jj].reshape(1, 128)
        else:
            common["win_%d" % l] = A(dsa_w_in)[jj]
            common["wout_%d" % l] = A(dsa_w_out)[jj]
    percore = []
    for cc in range(8):
        idx = special_bucket_idx(cc)
        ab = ext[idx]
        d = {"abias": np.ascontiguousarray(ab.transpose(1, 3, 0, 2)).reshape(128, 16 * 9 * 128)}
        d.update(dsa_percore(cfg, cc, ext) if L > 1 else {})
        percore.append(d)
    xs = shard_tokens_T(x[0], cfg)
    state = [{"xT_in": xs[cc]} for cc in range(8)]
    for k in range(L + 1):
        p = build_stage(cfg, k)
        maps = []
        for cc in range(8):
            m = dict(common)
            m.update(percore[cc])
            m.update(state[cc])
            maps.append(m)
        res = run_prog(p, maps)
        new = [{"xT_in": np.asarray(r["xT_out"])} for r in res]
        if k < L:
            if k % 2 == 0:
                kgath = np.concatenate([np.asarray(r["kx"]) for r in res], 0)
                vgath = np.concatenate([np.asarray(r["vx"]) for r in res], 0)
                for cc in range(8):
                    new[cc].update({"qT": np.asarray(res[cc]["qT"]), "ybx": np.asarray(res[cc]["ybx"]),
                                    "kg": kgath, "vg": vgath})
            else:
                dsa_exchange(res, new)
        state = new
    out = unshard_tokens_T([st["xT_in"] for st in state], cfg)
    return out[None].astype(np.float32)


def dsa_special_idx(c):
    sp = special_bucket_idx(c)
    idx = np.empty((12, 128, 128), np.int64)
    idx[0:3] = 31
    idx[3:12] = sp
    return idx


def dsa_percore(cfg, cc, ext):
    idx = dsa_special_idx(cc)
    db = ext[idx]
    im = np.zeros((128, 8, 128), np.float32)
    q = np.arange(128)[:, None]
    k = np.arange(128)[None, :]
    for r in range(8):
        if r > cc:
            im[:, r, :] = -1e30
        elif r == cc:
            im[:, r, :] = np.where(k <= q, 0.0, -1e30)
    return {"dbias": np.ascontiguousarray(db.transpose(1, 3, 0, 2)).reshape(128, 16 * 12 * 128),
            "imask": im.reshape(128, 1024)}


def dsa_exchange(res, new):
    g = lambda nm: np.concatenate([np.asarray(r[nm]) for r in res], 0)
    kcg, vcg, kig = g("kcx"), g("vcx"), g("kix")
    for cc in range(8):
        new[cc].update({"qcT": np.asarray(res[cc]["qcT"]), "qiT": np.asarray(res[cc]["qiT"]),
                        "wix": np.asarray(res[cc]["wix"]), "kcg": kcg, "vcg": vcg, "kig": kig})


def _dsa_methods():
    def dsa_declare_in(s):
        NT = s.cfg.NT
        s.din("qcT", [2048, NT], BF16)
        s.din("qiT", [1024, NT], BF16)
        s.din("wix", [NT, 16], F32)
        s.din("kcg", [8 * 512, NT], BF16)
        s.din("vcg", [8 * NT, 512], BF16)
        s.din("kig", [8 * 128, NT], BF16)

    def dsa_declare_out(s):
        NT = s.cfg.NT
        for nm, shp, dt in (("qcT", [2048, NT], BF16), ("qiT", [1024, NT], BF16), ("wix", [NT, 16], F32),
                            ("kcx", [512, NT], BF16), ("vcx", [NT, 512], BF16), ("kix", [128, NT], BF16)):
            s.dout(nm, shp, dt)

    def dsa_alloc(s, l):
        T, TB = s.cfg.T, s.cfg.TB
        s.qcT_sb = s.psb("qcT_sb", [128, 16, T], BF16)
        s.kcT_sb = s.psb("kcT_sb", [128, 4, T], BF16)
        s.vc_sb = s.psb("vc_sb", [128, TB, 512], BF16)
        s.qiT_sb = s.psb("qiT_sb", [128, 8, T], BF16)
        s.kiT_sb = s.psb("kiT_sb", [128, T], BF16)
        s.wi_sb = s.psb("wi_sb", [128, TB, 16], F32)
        s.dres_ = {k: Res() for k in ("qc", "kc", "vc", "qi", "ki", "wi")}

    def dsa_inproj(s, l, g):
        kb, cfg = s.kb, s.cfg
        T, TB, D, KC = cfg.T, cfg.TB, cfg.D, cfg.KC
        win = s.din("win_%d" % l, [D, 4176])
        R = s.dres_

        def pc(c0, n):
            return lambda k: [(0, n, win[k * 128:(k + 1) * 128, c0:c0 + n])]

        def copy_evac(dst, res, scale, n, width):
            def f(bs):
                for i in range(n):
                    b = bs[i]
                    if kb.ev() == "act":
                        kb.op("act", lambda e, i=i, b=b: e.activation(out=dst(i), in_=kb.banks[b][:, 0:width],
                                                                      func=AF.Identity, scale=scale),
                              reads=[kb.bres[b]], writes=[res])
                    else:
                        kb.op("dve", lambda e, i=i, b=b: e.tensor_scalar(out=dst(i), in0=kb.banks[b][:, 0:width],
                                                                         scalar1=scale, scalar2=None, op0=ALU.mult),
                              reads=[kb.bres[b]], writes=[res])
            return f
        for sg in range(4):
            linear(kb, s.wr, s.hT, s.hres, KC, T, pc(sg * 512, 512), "fm", 4,
                   copy_evac(lambda i, sg=sg: s.qcT_sb[:, sg * 4 + i, :], R["qc"], 128 ** -0.5, 4, T))
        linear(kb, s.wr, s.hT, s.hres, KC, T, pc(2048, 512), "fm", 4,
               copy_evac(lambda i: s.kcT_sb[:, i, :], R["kc"], 1.0, 4, T))
        linear(kb, s.wr, s.hT, s.hres, KC, T, pc(2560, 512), "tm", 512,
               copy_evac(lambda tb: s.vc_sb[:, tb, :], R["vc"], 1.0, TB, 512))
        for sg in range(2):
            linear(kb, s.wr, s.hT, s.hres, KC, T, pc(3072 + sg * 512, 512), "fm", 4,
                   copy_evac(lambda i, sg=sg: s.qiT_sb[:, sg * 4 + i, :], R["qi"], 1.0, 4, T))
        linear(kb, s.wr, s.hT, s.hres, KC, T,
               lambda k: [(0, 64, win[k * 128:(k + 1) * 128, 4096:4160]), (64, 64, win[k * 128:(k + 1) * 128, 4096:4160])],
               "fm", 1, copy_evac(lambda i: s.kiT_sb[:, :], R["ki"], 1.0, 1, T))
        linear(kb, s.wr, s.hT, s.hres, KC, T, pc(4160, 16), "tm", 16,
               copy_evac(lambda tb: s.wi_sb[:, tb, :], R["wi"], 64 ** -0.5, TB, 16))
        tok = slice(g * T, (g + 1) * T)
        for nm, sbt, rk in (("qcT", s.qcT_sb, "qc"), ("kcx", s.kcT_sb, "kc"), ("qiT", s.qiT_sb, "qi")):
            kb.dma("sp", s.d[nm].rearrange("(c p) t -> p c t", p=128)[:, :, tok], sbt[:], s.osem,
                   reads=[R[rk]], writes=[s.dres[nm]])
        kb.dma("sp", s.d["kix"][:, tok], s.kiT_sb[:], s.osem, reads=[R["ki"]], writes=[s.dres["kix"]])
        kb.dma("sp", s.d["vcx"].rearrange("(tb p) n -> p tb n", p=128)[:, g * TB:(g + 1) * TB, :], s.vc_sb[:], s.osem,
               reads=[R["vc"]], writes=[s.dres["vcx"]])
        kb.dma("sp", s.d["wix"].rearrange("(tb p) n -> p tb n", p=128)[:, g * TB:(g + 1) * TB, :], s.wi_sb[:], s.osem,
               reads=[R["wi"]], writes=[s.dres["wix"]])

    def dsa_attn(s, l):
        kb, cfg = s.kb, s.cfg
        NT, NB, TOPK = cfg.NT, cfg.NB, cfg.TOPK
        MARK = -3.0e38
        d = s.d
        dbias = s.din("dbias", [128, 16 * 12 * 128])
        imaskd = s.din("imask", [128, 1024])
        b31d = s.din("b31", [128, 16])
        s.phase_begin()
        LMAX = NB * 1024
        isc = s.psb("isc", [128, LMAX], F32)
        selT = s.psb("selT", [128, NB * 8, 128], BF16)
        db = s.psb("db", [128, 16 * 12 * 128], BF16)
        imask = s.psb("imask_sb", [128, 1024], F32)
        b31 = s.psb("b31_sb", [128, 16], F32)
        qi_j = s.psb("qi_j", [128, 8, 128], BF16)
        qc_j = s.psb("qc_j", [128, 16, 128], BF16)
        w_j = s.psb("w_j", [128, 16], F32)
        ya_j = PRot(s, "ya_j", [128, 16, 128], BF16, 2)
        mx = s.psb("mx", [128, 8], F32)
        fst = s.psb("dfst", [128, 8], F32)
        NR = 3
        ki_r = [s.psb("ki_r%d" % i, [128, 8, 128], BF16) for i in range(NR)]
        kc_r = [s.psb("kc_r%d" % i, [128, 8, 128], BF16) for i in range(NR)]
        va_r = [s.psb("va_r%d" % i, [128, 8, 130], BF16) for i in range(NR)]
        ki_res = [Res() for _ in range(NR)]
        kc_res = [Res() for _ in range(NR)]
        va_res = [Res() for _ in range(NR)]
        if not hasattr(s, "dsem"):
            s.dsem = {k: [kb.sem("d%s%d" % (k, i)) for i in range(NR)] for k in ("ki", "kc", "va")}
            s.qsem = kb.sem("dq")
        relu_t = PRot(s, "relu_t", [128, 512], F32, 3)
        selm = PRot(s, "selm", [128, 1024], BF16, 2)
        pT = PRot(s, "dpT", [128, 512], BF16, 4)
        ob = PRot(s, "dob", [128, 128], BF16, 2)
        rc, risc, rsel, rq, rmx, rf = Res(), Res(), Res(), Res(), Res(), Res()
        rya = None
        kb.dma("pool", db[:], dbias, s.csem, writes=[rc])
        kb.dma("sp", imask[:], imaskd, s.csem, writes=[rc])
        kb.dma("sp", b31[:], b31d, s.csem, writes=[rc])
        for i in range(NR):
            kb.op("dve", lambda e, i=i: e.memset(va_r[i][:, :, 128:130], 1.0), writes=[va_res[i]])
        kig4 = d["kig"].rearrange("(r f) (j t) -> f j r t", r=8, t=128)
        kcg4 = d["kcg"].rearrange("(r g f) (j t) -> g f j r t", r=8, g=4, t=128)
        vcg4 = d["vcg"].rearrange("(r j p) (g n) -> g p j r n", r=8, p=128, g=4)
        SB = [0, 1, 2, 3]
        cnt = {"ki": 0, "kv": 0, "sb": 0}
        for j in range(NB):
            L = (j + 1) * 1024
            qs = slice(j * 128, (j + 1) * 128)
            kb.dma("sp", qi_j[:], d["qiT"].rearrange("(c p) t -> p c t", p=128)[:, :, qs], s.qsem,
                   reads=[s.dres["qiT"]], writes=[rq])
            kb.dma("sp", qc_j[:], d["qcT"].rearrange("(c p) t -> p c t", p=128)[:, :, qs], s.qsem,
                   reads=[s.dres["qcT"]], writes=[rq])
            kb.dma("sp", w_j[:], d["wix"][qs, :], s.qsem, reads=[s.dres["wix"]], writes=[rq])
            for jp in range(j + 1):
                ri = cnt["ki"] % NR
                cnt["ki"] += 1
                kb.dma("sp", ki_r[ri][:], kig4[:, jp, :, :], s.dsem["ki"][ri], reads=[s.dres["kig"]], writes=[ki_res[ri]])
                for half in range(2):
                    col0 = jp * 1024 + half * 512
                    for h in range(16):
                        ps = slice((h % 2) * 64, (h % 2) * 64 + 64)
                        b = SB[cnt["sb"] % 4]
                        cnt["sb"] += 1
                        kb.op("pe", lambda e, b=b, ps=ps, h=h, ri=ri, half=half: e.matmul(
                            kb.banks[b][:, 0:512], qi_j[ps, h // 2, :],
                            ki_r[ri][ps, half * 4:(half + 1) * 4, :], start=True, stop=True),
                            reads=[rq, ki_res[ri]], writes=[kb.bres[b]])
                        rt, rr = relu_t.get()
                        kb.op("act", lambda e, b=b, rt=rt: e.activation(out=rt[:], in_=kb.banks[b][:, 0:512], func=AF.Relu),
                              reads=[kb.bres[b]], writes=[rr])
                        if h == 0:
                            kb.op("dve", lambda e, rt=rt, col0=col0: e.tensor_scalar(
                                out=isc[:, col0:col0 + 512], in0=rt[:], scalar1=w_j[:, 0:1], scalar2=None, op0=ALU.mult),
                                reads=[rr, rq], writes=[risc])
                        else:
                            kb.op("dve", lambda e, rt=rt, col0=col0, h=h: e.scalar_tensor_tensor(
                                out=isc[:, col0:col0 + 512], in0=rt[:], scalar=w_j[:, h:h + 1],
                                in1=isc[:, col0:col0 + 512], op0=ALU.mult, op1=ALU.add),
                                reads=[rr, rq, risc], writes=[risc])
            kb.op("dve", lambda e: e.tensor_tensor(out=isc[:, j * 1024:(j + 1) * 1024], in0=isc[:, j * 1024:(j + 1) * 1024],
                                                   in1=imask[:], op=ALU.add), reads=[risc, rc], writes=[risc])
            for it in range(TOPK // 8):
                kb.op("dve", lambda e: e.max(out=mx[:], in_=isc[:, 0:L]), reads=[risc], writes=[rmx])
                kb.op("dve", lambda e: e.match_replace(out=isc[:, 0:L], in_to_replace=mx[:], in_values=isc[:, 0:L],
                                                       imm_value=MARK), reads=[rmx, risc], writes=[risc])
            for jp in range(j + 1):
                sm, sr = selm.get()
                kb.op("dve", lambda e, sm=sm, jp=jp: e.tensor_scalar(
                    out=sm[:], in0=isc[:, jp * 1024:(jp + 1) * 1024], scalar1=-2.0e38, scalar2=NEG,
                    op0=ALU.is_gt, op1=ALU.mult), reads=[risc], writes=[sr])
                b = SB[cnt["sb"] % 4]
                cnt["sb"] += 1
                tps = kb.banks[b][:].bitcast(BF16)
                for r in range(8):
                    kb.op("pe", lambda e, r=r, sm=sm, tps=tps: e.transpose(
                        tps[:, r * 128:(r + 1) * 128], sm[:, r * 128:(r + 1) * 128], s.ident_bf[:]),
                        reads=[sr, s.cres], writes=[kb.bres[b]])
                kb.op("act", lambda e, tps=tps, jp=jp: e.activation(
                    out=selT[:, jp * 8:(jp + 1) * 8, :].rearrange("p a b -> p (a b)"), in_=tps[:, 0:1024],
                    func=AF.Identity), reads=[kb.bres[b]], writes=[rsel])
            ya_t, ya_r = ya_j.get()
            for g in range(4):
                OB = [4, 5, 6, 7]
                nblk = (j + 1) * 8
                for jp in range(j + 1):
                    ri = cnt["kv"] % NR
                    cnt["kv"] += 1
                    kb.dma("sp", kc_r[ri][:], kcg4[g, :, jp, :, :], s.dsem["kc"][ri], reads=[s.dres["kcg"]],
                           writes=[kc_res[ri]])
                    kb.dma("sp", va_r[ri][:, :, 0:128], vcg4[g, :, jp, :, :], s.dsem["va"][ri], reads=[s.dres["vcg"]],
                           writes=[va_res[ri]])
                    for half in range(2):
                        if jp == j:
                            p0 = 4 + half * 4
                        elif jp == j - 1 and half == 1:
                            p0 = 0
                        else:
                            p0 = None
                        for r4 in range(4):
                            hh = g * 4 + r4
                            b = SB[cnt["sb"] % 4]
                            cnt["sb"] += 1
                            kb.op("pe", lambda e, b=b, jp=jp, half=half: e.matmul(
                                kb.banks[b][:, 0:512], s.ident_bf[:],
                                selT[:, jp * 8 + half * 4:jp * 8 + half * 4 + 4, :], start=True, stop=False),
                                reads=[rsel, s.cres], writes=[kb.bres[b]])
                            if p0 is not None:
                                kb.op("pe", lambda e, b=b, hh=hh, p0=p0: e.matmul(
                                    kb.banks[b][:, 0:512], s.ident_bf[:],
                                    db[:, (hh * 12 + p0) * 128:(hh * 12 + p0 + 4) * 128], start=False, stop=False),
                                    reads=[rc, s.cres], writes=[kb.bres[b]])
                            for i in range(4):
                                kb.op("pe", lambda e, b=b, i=i, ri=ri, hh=hh, half=half: e.matmul(
                                    kb.banks[b][:, i * 128:(i + 1) * 128], kc_r[ri][:, half * 4 + i, :], qc_j[:, hh, :],
                                    start=False, stop=True, skip_group_check=True),
                                    reads=[kc_res[ri], rq], writes=[kb.bres[b]])
                            pt, pr = pT.get()
                            if p0 is None:
                                kb.op("act", lambda e, b=b, pt=pt, hh=hh: e.activation(
                                    out=pt[:], in_=kb.banks[b][:, 0:512], func=AF.Exp, bias=b31[:, hh:hh + 1]),
                                    reads=[kb.bres[b], rc], writes=[pr])
                            else:
                                kb.op("act", lambda e, b=b, pt=pt: e.activation(
                                    out=pt[:], in_=kb.banks[b][:, 0:512], func=AF.Exp), reads=[kb.bres[b]], writes=[pr])
                            for i in range(4):
                                bi = jp * 8 + half * 4 + i
                                kb.op("pe", lambda e, r4=r4, i=i, ri=ri, half=half, pt=pt, bi=bi: e.matmul(
                                    kb.banks[OB[r4]][:, 0:130], pt[:, i * 128:(i + 1) * 128], va_r[ri][:, half * 4 + i, :],
                                    start=(bi == 0), stop=(bi == nblk - 1)),
                                    reads=[pr, va_res[ri]], writes=[kb.bres[OB[r4]]])
                for r4 in range(4):
                    hh = g * 4 + r4
                    O = kb.banks[OB[r4]]
                    ro = kb.bres[OB[r4]]
                    kb.op("dve", lambda e, O=O: e.reciprocal(out=fst[:, 0:1], in_=O[:, 128:129]), reads=[ro], writes=[rf])
                    ob_t, ob_r = ob.get()
                    kb.op("dve", lambda e, O=O, ob_t=ob_t: e.tensor_scalar(out=ob_t[:], in0=O[:, 0:128], scalar1=fst[:, 0:1],
                                                                           scalar2=None, op0=ALU.mult),
                          reads=[ro, rf], writes=[ob_r])
                    b = SB[cnt["sb"] % 4]
                    cnt["sb"] += 1
                    tps = kb.banks[b][:].bitcast(BF16)
                    kb.op("pe", lambda e, ob_t=ob_t, tps=tps: e.transpose(tps[:, 0:128], ob_t[:], s.ident_bf[:]),
                          reads=[ob_r, s.cres], writes=[kb.bres[b]])
                    kb.op("act", lambda e, tps=tps, hh=hh, ya_t=ya_t: e.activation(out=ya_t[:, hh, :], in_=tps[:, 0:128],
                                                                                   func=AF.Identity),
                          reads=[kb.bres[b]], writes=[ya_r])
            kb.dma("sp", d["yaT"].rearrange("(c p) t -> p c t", p=128)[:, :, qs], ya_t[:], s.osem,
                   reads=[ya_r], writes=[s.dres["yaT"]])
        s.phase_end()

    Prog.dsa_declare_in = dsa_declare_in
    Prog.dsa_declare_out = dsa_declare_out
    Prog.dsa_alloc = dsa_alloc
    Prog.dsa_inproj = dsa_inproj
    Prog.dsa_attn = dsa_attn


_dsa_methods()
```
